# Optimizing a Trainium2 kernel written in Bass

```python
import jax, jax.numpy as jnp
from jax import lax
import numpy as np

D_MODEL = 2048
BATCH = 16
SEQ = 256
DEPTH = 2
DEC_BATCH = 2
DEC_SEQ = 2048
PAST_LEN = 512

GRID_W = 64
N_EVEN = (DEPTH + 1) // 2
N_ODD = DEPTH // 2
A_HEADS = 8
A_DK = 128
A_DV = 128
A_KDIM = A_HEADS * A_DK
A_VDIM = A_HEADS * A_DV
B_HEADS = 4
B_DK = 128
B_DV = 256
B_KDIM = B_HEADS * B_DK
B_VDIM = B_HEADS * B_DV
GLA_RANK = 16
GLA_GATE_NORM = 16.0
CHUNK = 32
EVEN_SPLITS = (A_KDIM, A_KDIM, A_KDIM, A_VDIM, A_VDIM, B_KDIM, B_KDIM, B_VDIM, B_VDIM, GLA_RANK, GLA_RANK)
EVEN_IN = sum(EVEN_SPLITS)
LRU_WIDTH = D_MODEL
LRU_BLOCKS = 8
LRU_BW = LRU_WIDTH // LRU_BLOCKS
LRU_C = 8.0
CONV_W = 4
CONV_PAD_L = 2
D_FF = (8 * D_MODEL + 3 * 256 - 1) // (3 * 256) * 256
EPS = 1e-6

kernel_name = 'hybrid_hgrn2_gla_rglru_prefix_diffusion_step'


def _offsets(sizes):
    return [int(v) for v in np.cumsum(sizes)[:-1]]


def _rms_norm(x, g):
    xf = x.astype(jnp.float32)
    y = xf * lax.rsqrt(jnp.mean(xf * xf, axis=-1, keepdims=True) + EPS)
    return (y * g.astype(jnp.float32)).astype(x.dtype)


def _modulation(cvec, w, b):
    m = jax.nn.silu(cvec) @ w + b
    return [u[:, None, :] for u in jnp.split(m, 6, axis=-1)]


def _chunk_gla(q, k, v, log_a, s0):
    bsz, t, h, _ = q.shape
    dv = v.shape[-1]
    n = t // CHUNK
    f32 = jnp.float32

    def to_chunks(z):
        return jnp.moveaxis(z.astype(f32).reshape(bsz, n, CHUNK, h, z.shape[-1]), 1, 0)

    qc, kc, vc, gc = to_chunks(q), to_chunks(k), to_chunks(v), to_chunks(log_a)
    causal = jnp.tril(jnp.ones((CHUNK, CHUNK), dtype=bool))

    def step(s, inp):
        qi, ki, vi, gi = inp
        b = jnp.cumsum(gi, axis=1)
        q_dec = qi * jnp.exp(b)
        k_inv = ki * jnp.exp(-b)
        att = jnp.where(causal, jnp.einsum('bihd,bjhd->bhij', q_dec, k_inv), 0.0)
        o = jnp.einsum('bihd,bhdv->bihv', q_dec, s) + jnp.einsum('bhij,bjhv->bihv', att, vi)
        b_last = b[:, -1]
        k_tail = ki * jnp.exp(b_last[:, None] - b)
        s = jnp.exp(b_last)[..., None] * s + jnp.einsum('bjhd,bjhv->bhdv', k_tail, vi)
        return s, o

    s_fin, o = lax.scan(step, s0.astype(f32), (qc, kc, vc, gc))
    return jnp.moveaxis(o, 0, 1).reshape(bsz, t, h, dv), s_fin


def _bidir_gla(q, k_f, k_b, v, g_f, g_b, s0):
    rev = lambda z: jnp.flip(z, axis=1)
    o_f, s_f = _chunk_gla(q, k_f, v, g_f, s0[:, 0])
    o_b, s_b = _chunk_gla(rev(q), rev(k_b), rev(v), rev(g_b), s0[:, 1])
    return o_f + rev(o_b), jnp.stack([s_f, s_b], axis=1)


def _gated_head_norm(o, gate, gain):
    bsz, t, h, dv = o.shape
    o = o * lax.rsqrt(jnp.mean(o * o, axis=-1, keepdims=True) + EPS) * gain.astype(jnp.float32)
    return o.reshape(bsz, t, h * dv) * jax.nn.silu(gate.astype(jnp.float32))


def _even_mixer(h, s_a0, s_b0, lb, w_in, w_alpha, b_alpha, a_gain, b_gain, w_out):
    bsz, t, _ = h.shape
    f32 = jnp.float32
    aq, af_f, af_b, ai, ag, bq, bk, bv, bg, blr_f, blr_b = jnp.split(h @ w_in, _offsets(EVEN_SPLITS), axis=-1)
    hd = lambda u, n: u.astype(f32).reshape(bsz, t, n, -1)
    lbf = lb.astype(f32)
    f_f = lbf + (1.0 - lbf) * jax.nn.sigmoid(af_f.astype(f32))
    f_b = lbf + (1.0 - lbf) * jax.nn.sigmoid(af_b.astype(f32))
    qa = hd(aq, A_HEADS) * (A_DK ** -0.5)
    o_a, s_a = _bidir_gla(qa, hd(1.0 - f_f, A_HEADS), hd(1.0 - f_b, A_HEADS), hd(ai, A_HEADS),
                          hd(jnp.log(f_f), A_HEADS), hd(jnp.log(f_b), A_HEADS), s_a0)
    g_f = jax.nn.log_sigmoid((blr_f @ w_alpha[0] + b_alpha[0]).astype(f32)) / GLA_GATE_NORM
    g_b = jax.nn.log_sigmoid((blr_b @ w_alpha[1] + b_alpha[1]).astype(f32)) / GLA_GATE_NORM
    qb = hd(bq, B_HEADS) * (B_DK ** -0.5)
    kb = hd(bk, B_HEADS)
    o_b, s_b = _bidir_gla(qb, kb, kb, hd(bv, B_HEADS), hd(g_f, B_HEADS), hd(g_b, B_HEADS), s_b0)
    merged = jnp.concatenate([_gated_head_norm(o_a, ag, a_gain), _gated_head_norm(o_b, bg, b_gain)], axis=-1)
    return merged.astype(h.dtype) @ w_out, s_a, s_b


def _centred_dwconv(u, w, b):
    n = u.shape[1]
    up = jnp.pad(u, ((0, 0), (CONV_PAD_L, CONV_W - 1 - CONV_PAD_L), (0, 0)))
    out = b
    for j in range(CONV_W):
        out = out + up[:, j:j + n] * w[j]
    return out


def _rglru(xc, w_rg, b_rg, w_ig, b_ig, lam, h0):
    bsz, t, _ = xc.shape
    f32 = jnp.float32
    xb = xc.reshape(bsz, t, LRU_BLOCKS, LRU_BW)
    r = jax.nn.sigmoid(jnp.einsum('btnc,ncd->btnd', xb, w_rg.astype(f32)).reshape(bsz, t, -1) + b_rg.astype(f32))
    i = jax.nn.sigmoid(jnp.einsum('btnc,ncd->btnd', xb, w_ig.astype(f32)).reshape(bsz, t, -1) + b_ig.astype(f32))
    log_a = -LRU_C * r * jax.nn.softplus(-lam.astype(f32))
    a = jnp.exp(log_a)
    u = jnp.sqrt(-jnp.expm1(2.0 * log_a)) * (i * xc)

    def combine(left, right):
        a1, b1 = left
        a2, b2 = right
        return a1 * a2, a2 * b1 + b2

    a_cum, hs = lax.associative_scan(combine, (a, u), axis=1)
    hs = hs + a_cum * h0.astype(f32)[:, None, :]
    return hs, hs[:, -1]


def _odd_mixer(h, s0, latent, w_in, conv_w, conv_b, rg_w, rg_b, ig_w, ig_b, lam, w_out):
    bsz, t, _ = h.shape
    f32 = jnp.float32
    xr, gate = jnp.split(h @ w_in, 2, axis=-1)
    if latent:
        rows = t // GRID_W
        xc = _centred_dwconv(xr.reshape(bsz * rows, GRID_W, LRU_WIDTH), conv_w, conv_b).reshape(bsz, t, LRU_WIDTH)
    else:
        xc = _centred_dwconv(xr, conv_w, conv_b)
    xc = xc.astype(f32)
    y_f, s_f = _rglru(xc, rg_w[0], rg_b[0], ig_w[0], ig_b[0], lam[0], s0[:, 0])
    y_b, s_b = _rglru(jnp.flip(xc, axis=1), rg_w[1], rg_b[1], ig_w[1], ig_b[1], lam[1], s0[:, 1])
    y = (y_f + jnp.flip(y_b, axis=1)) * jax.nn.gelu(gate.astype(f32), approximate=True)
    return y.astype(h.dtype) @ w_out, jnp.stack([s_f, s_b], axis=1)


def _swiglu(h, w_in, w_out):
    gt, up = jnp.split(h @ w_in, 2, axis=-1)
    return (jax.nn.silu(gt) * up) @ w_out


def _trunk(x, cvec, s_hgrn, s_gla, s_lru, latent, p):
    lb_all = jnp.cumsum(jax.nn.softmax(p['hgrn_lower_bounds'].astype(jnp.float32), axis=0), axis=0)
    out_h, out_g, out_r = [], [], []
    for l in range(DEPTH):
        sh1, sc1, g1, sh2, sc2, g2 = _modulation(cvec, p['w_mod'][l], p['b_mod'][l])
        h = _rms_norm(x, p['norm1_g'][l]) * (1 + sc1) + sh1
        e = l // 2
        if l % 2 == 0:
            mix, sa, sb = _even_mixer(h, s_hgrn[:, e], s_gla[:, e], lb_all[l], p['even_w_in'][e],
                                      p['gla_w_alpha'][e], p['gla_b_alpha'][e], p['hgrn_norm_g'][e],
                                      p['gla_norm_g'][e], p['even_w_out'][e])
            out_h.append(sa)
            out_g.append(sb)
        else:
            mix, sr = _odd_mixer(h, s_lru[:, e], latent, p['odd_w_in'][e], p['conv_w'][e], p['conv_b'][e],
                                 p['rg_w'][e], p['rg_b'][e], p['ig_w'][e], p['ig_b'][e],
                                 p['lru_lambda'][e], p['odd_w_out'][e])
            out_r.append(sr)
        x = x + g1 * mix
        h = _rms_norm(x, p['norm2_g'][l]) * (1 + sc2) + sh2
        x = x + g2 * _swiglu(h, p['ffn_w_in'][l], p['ffn_w_out'][l])
    y = _rms_norm(x, p['final_norm_g'])
    return y, jnp.stack(out_h, axis=1), jnp.stack(out_g, axis=1), jnp.stack(out_r, axis=1)


def setup_inputs(seed: int = 0) -> dict:
    key = jax.random.key(seed)
    ks = iter(jax.random.split(key, 40))
    f32 = jnp.float32
    nrm = lambda shape, scale: scale * jax.random.normal(next(ks), shape, f32)
    D = D_MODEL
    x_prompt = nrm((BATCH, SEQ, D), 1.0)
    x_sample = nrm((DEC_BATCH, DEC_SEQ, D), 1.0)
    state_hgrn = nrm((DEC_BATCH, N_EVEN, 2, A_HEADS, A_DK, A_DV), 0.5)
    state_gla = nrm((DEC_BATCH, N_EVEN, 2, B_HEADS, B_DK, B_DV), 1.0)
    state_rglru = nrm((DEC_BATCH, N_ODD, 2, LRU_WIDTH), 0.5)
    c = nrm((DEC_BATCH, D), 1.0)
    c_ctx = nrm((D,), 1.0)
    norm1_g = 1.0 + nrm((DEPTH, D), 0.05)
    norm2_g = 1.0 + nrm((DEPTH, D), 0.05)
    w_mod = nrm((DEPTH, D, 6 * D), D ** -0.5)
    b_mod = nrm((DEPTH, 6 * D), 0.01)
    ffn_w_in = nrm((DEPTH, D, 2 * D_FF), D ** -0.5)
    ffn_w_out = nrm((DEPTH, D_FF, D), D_FF ** -0.5)
    hgrn_lower_bounds = nrm((DEPTH + 1, A_KDIM), 0.1)
    even_w_in = nrm((N_EVEN, D, EVEN_IN), D ** -0.5)
    gla_w_alpha = nrm((N_EVEN, 2, GLA_RANK, B_KDIM), GLA_RANK ** -0.5)
    gla_b_alpha = nrm((N_EVEN, 2, B_KDIM), 0.1)
    hgrn_norm_g = 1.0 + nrm((N_EVEN, A_DV), 0.05)
    gla_norm_g = 1.0 + nrm((N_EVEN, B_DV), 0.05)
    even_w_out = nrm((N_EVEN, A_VDIM + B_VDIM, D), (A_VDIM + B_VDIM) ** -0.5)
    odd_w_in = nrm((N_ODD, D, 2 * LRU_WIDTH), D ** -0.5)
    conv_w = nrm((N_ODD, CONV_W, LRU_WIDTH), CONV_W ** -0.5)
    conv_b = nrm((N_ODD, LRU_WIDTH), 0.01)
    rg_w = nrm((N_ODD, 2, LRU_BLOCKS, LRU_BW, LRU_BW), LRU_BW ** -0.5)
    rg_b = nrm((N_ODD, 2, LRU_WIDTH), 0.01)
    ig_w = nrm((N_ODD, 2, LRU_BLOCKS, LRU_BW, LRU_BW), LRU_BW ** -0.5)
    ig_b = nrm((N_ODD, 2, LRU_WIDTH), 0.01)
    a_c = jax.random.uniform(next(ks), (N_ODD, 2, LRU_WIDTH), f32, 0.9, 0.999)
    a_base = a_c ** (1.0 / LRU_C)
    lru_lambda = jnp.log(a_base) - jnp.log1p(-a_base)
    odd_w_out = nrm((N_ODD, LRU_WIDTH, D), LRU_WIDTH ** -0.5)
    final_norm_g = 1.0 + nrm((D,), 0.05)
    return {'x_prompt': x_prompt, 'x_sample': x_sample, 'state_hgrn': state_hgrn, 'state_gla': state_gla,
            'state_rglru': state_rglru, 'c': c, 'c_ctx': c_ctx, 'norm1_g': norm1_g, 'norm2_g': norm2_g,
            'w_mod': w_mod, 'b_mod': b_mod, 'ffn_w_in': ffn_w_in, 'ffn_w_out': ffn_w_out,
            'hgrn_lower_bounds': hgrn_lower_bounds, 'even_w_in': even_w_in, 'gla_w_alpha': gla_w_alpha,
            'gla_b_alpha': gla_b_alpha, 'hgrn_norm_g': hgrn_norm_g, 'gla_norm_g': gla_norm_g,
            'even_w_out': even_w_out, 'odd_w_in': odd_w_in, 'conv_w': conv_w, 'conv_b': conv_b,
            'rg_w': rg_w, 'rg_b': rg_b, 'ig_w': ig_w, 'ig_b': ig_b, 'lru_lambda': lru_lambda,
            'odd_w_out': odd_w_out, 'final_norm_g': final_norm_g}


def reference(x_prompt, x_sample, state_hgrn, state_gla, state_rglru, c, c_ctx, norm1_g, norm2_g, w_mod, b_mod,
              ffn_w_in, ffn_w_out, hgrn_lower_bounds, even_w_in, gla_w_alpha, gla_b_alpha, hgrn_norm_g,
              gla_norm_g, even_w_out, odd_w_in, conv_w, conv_b, rg_w, rg_b, ig_w, ig_b, lru_lambda, odd_w_out,
              final_norm_g):
    p = dict(norm1_g=norm1_g, norm2_g=norm2_g, w_mod=w_mod, b_mod=b_mod, ffn_w_in=ffn_w_in,
             ffn_w_out=ffn_w_out, hgrn_lower_bounds=hgrn_lower_bounds, even_w_in=even_w_in,
             gla_w_alpha=gla_w_alpha, gla_b_alpha=gla_b_alpha, hgrn_norm_g=hgrn_norm_g, gla_norm_g=gla_norm_g,
             even_w_out=even_w_out, odd_w_in=odd_w_in, conv_w=conv_w, conv_b=conv_b, rg_w=rg_w, rg_b=rg_b,
             ig_w=ig_w, ig_b=ig_b, lru_lambda=lru_lambda, odd_w_out=odd_w_out, final_norm_g=final_norm_g)
    bp = x_prompt.shape[0]
    z_hgrn = jnp.zeros((bp, N_EVEN, 2, A_HEADS, A_DK, A_DV), jnp.float32)
    z_gla = jnp.zeros((bp, N_EVEN, 2, B_HEADS, B_DK, B_DV), jnp.float32)
    z_lru = jnp.zeros((bp, N_ODD, 2, LRU_WIDTH), jnp.float32)
    y_prompt, new_hgrn, new_gla, new_lru = _trunk(x_prompt, c_ctx[None, :], z_hgrn, z_gla, z_lru, False, p)
    y_sample = _trunk(x_sample, c, state_hgrn, state_gla, state_rglru, True, p)[0]
    dt = x_prompt.dtype
    return (y_prompt, y_sample, new_hgrn.astype(dt), new_gla.astype(dt), new_lru.astype(dt))
```

```python
import numpy as np
from contextlib import ExitStack
import concourse.bass as bass
import concourse.mybir as mybir
from concourse.bass_utils import run_bass_kernel_spmd

F32 = mybir.dt.float32
BF16 = mybir.dt.bfloat16
AF = mybir.ActivationFunctionType
ALU = mybir.AluOpType

D = 2048
NT = 1024
HALF = 512
DFF = 5632
EPS = 1e-6
NCORES = 8


class Prog:
    ENG = ["pe", "act", "dve", "pool", "sp"]

    def __init__(self, ndma=40):
        self.ops = {e: [] for e in self.ENG}
        self.cnt = {e: 0 for e in self.ENG}
        self.pending = {e: False for e in self.ENG}
        self.waited = {e: {} for e in self.ENG}
        self.res = {}
        self.ndma = ndma
        self.dma_n = 0
        self.dma_np = 0
        self.dma_ns = 0
        self.nsw = 12
        self.dma_val = [0] * ndma
        self.ncc = 0
        self.fence = None

    def _collect(self, eng, reads, writes, extra=()):
        deps = {}

        def add(d):
            if d is not None:
                k, v = d
                if v > deps.get(k, 0):
                    deps[k] = v
        for r in reads:
            st = self.res.get(r)
            if st:
                add(st[0])
        for w in writes:
            st = self.res.get(w)
            if st:
                add(st[0])
                for k, v in st[1].items():
                    add((k, v))
        for d in extra:
            add(d)
        waits = []
        for k, v in deps.items():
            if k == eng and eng == "pe":
                continue
            if k in self.cnt and v > self.cnt[k]:
                raise RuntimeError(f"op on {eng} depends on a not-yet-signalled op of {k} ({v} > {self.cnt[k]})")
            if self.waited[eng].get(k, 0) >= v:
                continue
            self.waited[eng][k] = v
            waits.append((k, v))
        return waits

    def _record(self, myid, reads, writes):
        for r in reads:
            st = self.res.setdefault(r, [None, {}])
            if myid[1] > st[1].get(myid[0], 0):
                st[1][myid[0]] = myid[1]
        for w in writes:
            self.res[w] = [myid, {}]

    def op(self, eng, fn, reads=(), writes=(), signal=True):
        waits = self._collect(eng, reads, writes)
        if signal:
            self.cnt[eng] += 1
            myid = (eng, self.cnt[eng])
            inc = (eng, 1)
            self.pending[eng] = False
        else:
            myid = (eng, self.cnt[eng] + 1)
            inc = None
            self.pending[eng] = True
        self.ops[eng].append((fn, waits, inc))
        self._record(myid, reads, writes)

    def dma(self, eng, fn, reads=(), writes=()):
        if eng == "pool":
            i = self.dma_np % self.nsw
            self.dma_np += 1
        else:
            i = self.nsw + (self.dma_ns % (self.ndma - self.nsw))
            self.dma_ns += 1
        self.dma_n += 1
        extra = []
        if self.dma_val[i] > 0:
            extra.append((("dma", i), self.dma_val[i]))
        if self.fence is not None:
            extra.append(self.fence)
        waits = self._collect(eng, reads, writes, extra)
        self.dma_val[i] += 16
        myid = (("dma", i), self.dma_val[i])
        self.ops[eng].append((fn, waits, (("dma", i), 16)))
        self._record(myid, reads, writes)

    def collective(self, fn, reads=(), writes=()):
        extra = [(("dma", i), self.dma_val[i]) for i in range(self.ndma) if self.dma_val[i] > 0]
        waits = self._collect("pool", reads, writes, extra)
        k = ("cc", self.ncc)
        self.ncc += 1
        self.ops["pool"].append((fn, waits, (k, None)))
        self._record((k, 1), reads, writes)

    def finish(self):
        waits = []
        for i in range(self.ndma):
            if self.dma_val[i] > 0 and self.waited["sp"].get(("dma", i), 0) < self.dma_val[i]:
                waits.append((("dma", i), self.dma_val[i]))
        for e in ["pe", "act", "dve", "pool"]:
            assert not self.pending[e], e
            if self.cnt[e] > 0:
                waits.append((e, self.cnt[e]))
        self.ops["sp"].append((None, waits, None))

    def emit(self, nc, block, sems):
        engs = {"pe": block.tensor, "act": block.scalar, "dve": block.vector,
                "pool": block.gpsimd, "sp": block.sync}
        for name in self.ENG:
            ops = self.ops[name]

            def body(e, ops=ops):
                for fn, waits, inc in ops:
                    for k, v in waits:
                        e.wait_ge(sems[k], v)
                    if fn is None:
                        continue
                    ins = fn(e)
                    if inc is not None:
                        if inc[1] is None:
                            ins.then_inc(sems[inc[0]])
                        else:
                            ins.then_inc(sems[inc[0]], inc[1])
            engs[name](body)


A_COLS = dict(aq=0, af_f=1024, af_b=2048, ai=3072, ag=4096)
B_COLS = dict(bq=5120, bk=5632, bv=6144, bg=7168, blr=8192)


def build_program(stage=99):
    nc = bass.Bass("TRN2", target_bir_lowering=False)
    P = Prog()
    es = ExitStack()

    def din(name, shape, dt=F32):
        return nc.dram_tensor(name, list(shape), dt, kind="ExternalInput").ap()

    def dout(name, shape, dt=F32):
        return nc.dram_tensor(name, list(shape), dt, kind="ExternalOutput").ap()

    _ins = {}
    _spec = {
        "xin": [NT, D],
        "cvT": [128, 48],
        "sel": [128, 2],
        "msk": [128, 16],
        "st_a": [2, 8, 128, 128],
        "st_b": [2, 4, 128, 256],
        "st_r": [128, 32],
        "wmod": [2, D, 1536],
        "bmod": [128, 24],
        "n1g": [128, 32],
        "n2g": [128, 32],
        "fng": [128, 16],
        "lbp": [128, 24],
        "w_in0": [D, 8224],
        "w_al": [2, 16, 512],
        "b_al": [128, 8],
        "a_gn": [128, 1],
        "b_gn": [128, 2],
        "w_out0": [D, D],
        "w_in1": [D, 2 * D],
        "cw": [128, 64],
        "cb": [128, 16],
        "rgw": [2, 8, 256, 256],
        "rgb": [128, 32],
        "igw": [2, 8, 256, 256],
        "igb": [128, 32],
        "lam": [128, 32],
        "w_out1": [D, D],
        "fw_in": [2, D, 2 * DFF],
        "fw_out": [2, DFF, D],
        "c_idb": [128, 128],
        "c_mf": [128, 512],
        "c_mb": [128, 512],
        "c_rf": [128, 512],
        "c_rb": [128, 512],
        "c_rm": [128, 4],
    }

    class _Lazy:
        def __init__(self, name):
            self.name = name

        def ap(self):
            if self.name not in _ins:
                _ins[self.name] = nc.dram_tensor(self.name, list(_spec[self.name]), F32, kind="ExternalInput").ap()
            return _ins[self.name]

        def __getitem__(self, k):
            return self.ap()[k]

        def rearrange(self, *a, **k):
            return self.ap().rearrange(*a, **k)

    xin = _Lazy("xin")
    cvT = _Lazy("cvT")
    sel = _Lazy("sel")
    msk = _Lazy("msk")
    st_a = _Lazy("st_a")
    st_b = _Lazy("st_b")
    st_r = _Lazy("st_r")
    wmod = _Lazy("wmod")
    bmod = _Lazy("bmod")
    n1g = _Lazy("n1g")
    n2g = _Lazy("n2g")
    fng = _Lazy("fng")
    lbp = _Lazy("lbp")
    w_in0 = _Lazy("w_in0")
    w_al = _Lazy("w_al")
    b_al = _Lazy("b_al")
    a_gn = _Lazy("a_gn")
    b_gn = _Lazy("b_gn")
    w_out0 = _Lazy("w_out0")
    w_in1 = _Lazy("w_in1")
    cw = _Lazy("cw")
    cb = _Lazy("cb")
    rgw = _Lazy("rgw")
    rgb = _Lazy("rgb")
    igw = _Lazy("igw")
    igb = _Lazy("igb")
    lam = _Lazy("lam")
    w_out1 = _Lazy("w_out1")
    fw_in = _Lazy("fw_in")
    fw_out = _Lazy("fw_out")
    c_idb = _Lazy("c_idb")
    c_mf = _Lazy("c_mf")
    c_mb = _Lazy("c_mb")
    c_rf = _Lazy("c_rf")
    c_rb = _Lazy("c_rb")
    c_rm = _Lazy("c_rm")

    full = stage >= 99
    if full:
        y_out = dout("y", [NT, D])
        nsa = dout("nsa", [2, 2, 8, 128, 128])
        nsb = dout("nsb", [2, 2, 4, 128, 256])
        nsr = dout("nsr", [128, 64])
    else:
        y_out = nc.dram_tensor("y_i", [NT, D], F32).ap()
        nsa = nc.dram_tensor("nsa_i", [2, 2, 8, 128, 128], F32).ap()
        nsb = nc.dram_tensor("nsb_i", [2, 2, 4, 128, 256], F32).ap()
        nsr = nc.dram_tensor("nsr_i", [128, 64], F32).ap()
        dbg_small = dout("dbg_small", [128, 1664])
        dbg_h = dout("dbg_h", [128, 16 * NT], BF16)
        dbg_x = dout("dbg_x", [128, 16 * NT])
        dbg_q = dout("dbg_q", [128, 32 * HALF], BF16)

    SUMW = 4096 + 64
    bn_mod_i = nc.dram_tensor("bn_mod_i", [128, 128], F32)
    bn_mod_o = nc.dram_tensor("bn_mod_o", [NCORES * 128, 128], F32)
    bn_s0_i = nc.dram_tensor("bn_s0_i", [128, SUMW], F32)
    bn_s0_o = nc.dram_tensor("bn_s0_o", [NCORES * 128, SUMW], F32)
    bn_s1_i = nc.dram_tensor("bn_s1_i", [128, 64], F32)
    bn_s1_o = nc.dram_tensor("bn_s1_o", [NCORES * 128, 64], F32)
    xspill = nc.dram_tensor("xspill", [128, 16 * NT], F32)

    def sb(name, shape, dt=F32):
        return es.enter_context(nc.sbuf_tensor(name, list(shape), dt))

    def ps(name, shape, dt=F32):
        return es.enter_context(nc.psum_tensor(name, list(shape), dt))

    xT = sb("xT", [128, 16, NT])
    hT = sb("hT", [128, 16, NT], BF16)
    wb = [sb(f"wb{i}", [128, 16, 128], BF16) for i in range(4)]
    qst = sb("qst", [128, 32, HALF], BF16)
    Qd = sb("Qd", [128, HALF], BF16)
    Ki = sb("Ki", [128, HALF], BF16)
    Kt = sb("Kt", [128, HALF], BF16)
    KtM = sb("KtM", [128, 4, 512], BF16)
    vT2 = sb("vT2", [128, 2, HALF], BF16)
    vTk = sb("vTk", [128, 2, 512], BF16)
    attm = sb("attm", [128, HALF], BF16)
    Sst = sb("Sst", [128, 16, 128], BF16)
    S32 = [sb(f"S32_{i}", [128, 128]) for i in range(2)]
    osb = sb("osb", [128, 2, HALF])
    sgt = sb("sgt", [128, 2, HALF])
    rsT = sb("rsT", [128, HALF])
    blrT = sb("blrT", [32, HALF])
    wal = sb("wal", [32, 2, 512])
    sqb = [sb(f"sqb{i}", [128, HALF], BF16) for i in range(2)]
    xcb = sb("xcb", [128, 2, NT], BF16)
    small = sb("small", [128, 1664])
    ident = sb("ident", [128, 128])
    identb = sb("identb", [128, 128], BF16)
    onesb = sb("onesb", [128, 128], BF16)
    mF = sb("mF", [128, 512])
    mB = sb("mB", [128, 512])
    rF = sb("rF", [128, 512])
    rB = sb("rB", [128, 512])
    ones32 = sb("ones32", [128, 512])
    dummy = sb("dummy_t", [128, 8])

    pj0 = ps("pj0", [128, 512])
    pj1 = ps("pj1", [128, 512])
    p_att = ps("p_att", [128, 512])
    p_o = ps("p_o", [128, 512])
    p_u = [ps(f"p_u{i}", [128, 512]) for i in range(2)]
    p_tr = ps("p_tr", [128, 1024], BF16)
    p_x = ps("p_x", [128, 512])
    PUK = [[f"p_u{b}"] for b in range(2)]
    BANKS2 = [(pj0, ["pj0"]), (pj1, ["pj1"])]
    BANKS6 = BANKS2 + [(p_att, ["p_att"]), (p_o, ["p_o"]), (p_u[0], PUK[0]), (p_u[1], PUK[1])]

    xflat = xT[:].rearrange("p c t -> p (c t)")
    qflat = qst[:].rearrange("p a t -> p (a t)")
    mT = qflat.rearrange("p (c t) -> p c t", t=NT)
    qf32 = qflat.bitcast(F32)
    STG = [qf32[:, 4096:6144], qf32[:, 6144:8192]]
    STGK = ["stg0", "stg1"]

    _so = [0]

    def salloc(n):
        o = _so[0]
        _so[0] += n
        assert _so[0] <= 1664
        return small[:, o:o + n]

    cv_s = salloc(48); sel_s = salloc(2); msk_s = salloc(16); bmod_s = salloc(24)
    n1g_s = salloc(32); n2g_s = salloc(32); fng_s = salloc(16); lb_s = salloc(24)
    bal_s = salloc(8); agn_s = salloc(1); bgn_s = salloc(2); cw_s = salloc(64)
    cb_s = salloc(16); rgb_s = salloc(32); igb_s = salloc(32); lam_s = salloc(32)
    str_s = salloc(32); rm_s = salloc(4); eps_s = salloc(1); one_s = salloc(1)
    modloc_f = salloc(128); modloc = modloc_f[:, 0:72]; modP = salloc(192); modS = salloc(192); affA = salloc(128)
    lb0 = salloc(8); oml = salloc(8); noml = salloc(8); nbal = salloc(8)
    c1_s = salloc(32); c2_s = salloc(32); dvec = salloc(64)
    r1sum = salloc(64); hin = salloc(32); tmpv = salloc(64); nsrs = salloc(64)

    def act(out, in_, func, reads, writes, bias=None, scale=None):
        kw = {}
        if bias is not None:
            kw["bias"] = bias
        if scale is not None:
            kw["scale"] = scale
        P.op("act", lambda e: e.activation(out=out, in_=in_, func=func, **kw), reads, writes)

    def tt(eng, out, in0, in1, op, reads, writes):
        P.op(eng, lambda e: e.tensor_tensor(out=out, in0=in0, in1=in1, op=op), reads, writes)

    def ts(eng, out, in0, s1, s2, op0, op1, reads, writes):
        if op1 is None:
            P.op(eng, lambda e: e.tensor_scalar(out=out, in0=in0, scalar1=s1, scalar2=None, op0=op0), reads, writes)
        else:
            P.op(eng, lambda e: e.tensor_scalar(out=out, in0=in0, scalar1=s1, scalar2=s2, op0=op0, op1=op1), reads, writes)

    def stt(out, in0, scalar, in1, op0, op1, reads, writes):
        P.op("dve", lambda e: e.scalar_tensor_tensor(out=out, in0=in0, scalar=scalar, in1=in1, op0=op0, op1=op1), reads, writes)

    def scan(out, d0, d1, init, reads, writes, op0=ALU.mult, op1=ALU.add):
        P.op("dve", lambda e: e.tensor_tensor_scan(out=out, data0=d0, data1=d1, initial=init, op0=op0, op1=op1), reads, writes)

    def cp(eng, out, in_, reads, writes):
        if eng == "act":
            act(out, in_, AF.Copy, reads, writes)
        else:
            P.op(eng, lambda e: e.tensor_copy(out=out, in_=in_), reads, writes)

    def recip(out, in_, reads, writes):
        P.op("dve", lambda e: e.reciprocal(out=out, in_=in_), reads, writes)

    def mset(eng, ap, val, writes):
        P.op(eng, lambda e: e.memset(ap, val), (), writes)

    def mm(out, lhsT, rhs, start, stop, reads, writes, signal):
        P.op("pe", lambda e: e.matmul(out=out, lhsT=lhsT, rhs=rhs, start=start, stop=stop), reads, writes, signal)

    def tr(out, in_, idn, reads, writes, signal):
        P.op("pe", lambda e: e.transpose(out=out, in_=in_, identity=idn), reads, writes, signal)

    def dma(eng, out, in_, reads, writes):
        if isinstance(in_, _Lazy):
            in_ = in_.ap()
        P.dma(eng, lambda e: e.dma_start(out=out, in_=in_), reads, writes)

    import os as _os2
    NOCC = _os2.environ.get("KNOCC", "") == "1"

    def allgather(src, dst, rk, wk):
        if NOCC or stage == 0.36:
            for r_ in range(NCORES):
                dma("pool", dst.ap()[r_ * 128:(r_ + 1) * 128, :], src.ap(), rk, wk)
        else:
            P.collective(lambda e: e.collective_compute("AllGather", ALU.bypass, replica_groups=[list(range(NCORES))],
                                                        ins=[src.ap().opt()], outs=[dst.ap().opt()]), rk, wk)
            P.op("pool", lambda e: e.memset(dummy[:, 4:5], 0.0), wk, ["dummy_cc"])
            P.fence = ("pool", P.cnt["pool"])

    def barrier(keys):
        keys = list(keys)
        P.op("dve", lambda e: e.memset(dummy[:, 0:1], 0.0), keys, keys + ["dummy"])

    XK = [f"xT{i}" for i in range(4)]
    TK = [f"T{i}" for i in range(8)]
    QK = [f"qst{i}" for i in range(32)]

    class WStream:
        def __init__(self):
            self.plan = []
            self.nl = 0
            self.nu = 0

        def extend(self, slabs):
            self.plan += slabs

        def get(self):
            i = self.nu
            self.nu += 1
            while self.nl < len(self.plan) and self.nl <= i + 3:
                ap, nk, nco = self.plan[self.nl]
                s = self.nl % 4
                dma("pool", wb[s][:, 0:nk, 0:nco], ap.rearrange("(c p) n -> p c n", p=128), [], [f"wb{s}"])
                self.nl += 1
            ap, nk, nco = self.plan[i]
            return i % 4, nk, nco

    WS = WStream()

    def projm(targets, N=HALF):
        s, nk, nco = WS.get()
        for (rhs_fn, out_ps, rkeys, wkeys) in targets:
            for kc in range(nk):
                mm(out_ps[0:nco, 0:N], wb[s][:, kc, 0:nco], rhs_fn(kc), kc == 0, kc == nk - 1,
                   [f"wb{s}"] + list(rkeys), list(wkeys), kc == nk - 1)

    def w_slab(w, col, ncols=128, k0=0, nk=16):
        return (w[k0 * 128:(k0 + nk) * 128, col:col + ncols], nk, ncols)

    mset("dve", small[:], 0.0, ["small"])
    if not full:
        mset("dve", qflat, 0.0, QK + STGK)
        mset("dve", hT[:].rearrange("p c t -> p (c t)"), 0.0, ["hT0", "hT1"])
        mset("dve", xflat, 0.0, XK)
    for dst, src in [(cv_s, cvT), (sel_s, sel), (msk_s, msk), (bmod_s, bmod), (n1g_s, n1g), (n2g_s, n2g),
                     (fng_s, fng), (lb_s, lbp), (bal_s, b_al), (agn_s, a_gn), (bgn_s, b_gn), (cw_s, cw),
                     (cb_s, cb), (rgb_s, rgb), (igb_s, igb), (lam_s, lam), (str_s, st_r), (rm_s, c_rm)]:
        dma("sp", dst, src, [], ["small"])
    dma("sp", ident[:], c_idb, [], ["ident"])
    dma("pool", identb[:], c_idb, [], ["identb"])
    dma("sp", mF[:], c_mf, [], ["mF"])
    dma("sp", mB[:], c_mb, [], ["mB"])
    dma("sp", rF[:], c_rf, [], ["rF"])
    dma("sp", rB[:], c_rb, [], ["rB"])
    mset("dve", onesb[:], 1.0, ["onesb"])
    mset("dve", ones32[:], 1.0, ["ones32"])
    mset("dve", eps_s, EPS, ["small"])
    mset("dve", one_s, 1.0, ["small"])
    mset("dve", wal[:].rearrange("p d n -> p (d n)"), 0.0, ["wal"])
    dma("sp", wal[0:16, 0, :], w_al[0], [], ["wal"])
    dma("sp", wal[16:32, 1, :], w_al[1], [], ["wal"])

    modP3 = modP.rearrange("p (l g) -> p l g", l=2)
    modS3 = modS.rearrange("p (l g) -> p l g", l=2)
    modG = [modP3, modS3]
    affA4 = affA.rearrange("p (g l n c) -> p g l n c", g=2, l=2, n=2)

    def mod_vec(g, l, j):
        return modG[g][:, l, j * 16:(j + 1) * 16]

    def modulation():
        scv = tmpv[:, 0:48]
        act(scv, cv_s, AF.Silu, ["small"], ["small"])
        scv3 = scv.rearrange("p (c v) -> p c v", v=3)
        wmb = [xflat[:, 0:2048].rearrange("p (c n) -> p c n", n=128),
               xflat[:, 2048:4096].rearrange("p (c n) -> p c n", n=128)]
        modps = p_x[:, 0:72].rearrange("p (b v) -> p b v", v=3)
        for lb_i in range(24):
            l, b = divmod(lb_i, 12)
            s = lb_i % 2
            dma("sp", wmb[s], wmod[l][:, b * 128:(b + 1) * 128].rearrange("(c p) n -> p c n", p=128), [], [f"wm{s}"])
            for kc in range(16):
                mm(modps[:, lb_i, :], wmb[s][:, kc, :], scv3[:, kc, :], kc == 0, kc == 15,
                   [f"wm{s}", "small"], ["p_x"], kc == 15)
        tt("dve", modloc.rearrange("p (b v) -> p b v", v=3), modps,
           bmod_s.unsqueeze(2).to_broadcast([128, 24, 3]), ALU.add, ["p_x", "small"], ["small"])
        dma("pool", bn_mod_i.ap(), modloc_f, ["small"], ["bn_mod_i"])
        allgather(bn_mod_i, bn_mod_o, ["bn_mod_i"], ["bn_mod_o"])
        Gm = xflat[:, 4096:4672]
        dma("pool", Gm.rearrange("p (r f) -> p r f", f=72), bn_mod_o.ap().rearrange("(r p) f -> p r f", p=128)[:, :, 0:72],
            ["bn_mod_o"], ["Gm"])
        G5 = Gm.rearrange("p (r l b v) -> p r l b v", l=2, b=12, v=3)
        moda = xflat[:, 5120:5696]
        modall = moda.rearrange("p (l v r b) -> p l v r b", l=2, v=3, r=8)
        for l in range(2):
            for v in range(3):
                cp("dve", modall[:, l, v, :, :], G5[:, :, l, :, v], ["Gm"], ["modall"])
        modall2 = moda.rearrange("p (l v g) -> p l v g", l=2, v=3)
        modP3 = modP.rearrange("p (l g) -> p l g", l=2)
        modS3 = modS.rearrange("p (l g) -> p l g", l=2)
        for l in range(2):
            cp("dve", modP3[:, l, :], modall2[:, l, 0, :], ["modall"], ["small"])
            ts("dve", modS3[:, l, :], modall2[:, l, 1, :], sel_s[:, 0:1], None, ALU.mult, None, ["modall", "small"], ["small"])
            stt(modS3[:, l, :], modall2[:, l, 2, :], sel_s[:, 1:2], modS3[:, l, :], ALU.mult, ALU.add, ["modall", "small"], ["small"])
        modG = [modP3, modS3]
        affA4 = affA.rearrange("p (g l n c) -> p g l n c", g=2, l=2, n=2)
        ngs = [n1g_s.rearrange("p (l c) -> p l c", l=2), n2g_s.rearrange("p (l c) -> p l c", l=2)]
        for g in range(2):
            for l in range(2):
                for n in range(2):
                    sc = modG[g][:, l, (1 + 3 * n) * 16:(2 + 3 * n) * 16]
                    stt(affA4[:, g, l, n, :], sc, 1.0, ngs[n][:, l, :], ALU.add, ALU.mult, ["small"], ["small"])


        lb3 = lb_s.rearrange("p (h t) -> p h t", t=3)
        e3 = tmpv[:, 0:24].rearrange("p (h t) -> p h t", t=3)
        act(tmpv[:, 0:24], lb_s, AF.Exp, ["small"], ["small"])
        tt("dve", tmpv[:, 24:32], e3[:, :, 0], e3[:, :, 1], ALU.add, ["small"], ["small"])
        tt("dve", tmpv[:, 24:32], tmpv[:, 24:32], e3[:, :, 2], ALU.add, ["small"], ["small"])
        recip(tmpv[:, 32:40], tmpv[:, 24:32], ["small"], ["small"])
        tt("dve", lb0, e3[:, :, 0], tmpv[:, 32:40], ALU.mult, ["small"], ["small"])
        ts("dve", oml, lb0, -1.0, 1.0, ALU.mult, ALU.add, ["small"], ["small"])
        ts("dve", noml, oml, -1.0, None, ALU.mult, None, ["small"], ["small"])
        ts("dve", nbal, bal_s, -1.0, None, ALU.mult, None, ["small"], ["small"])
        act(tmpv[:, 0:32], lam_s, AF.Exp, ["small"], ["small"], scale=-1.0)
        act(tmpv[:, 0:32], tmpv[:, 0:32], AF.Ln, ["small"], ["small"], bias=one_s)
        ts("dve", c1_s, tmpv[:, 0:32], -8.0, None, ALU.mult, None, ["small"], ["small"])
        ts("dve", c2_s, tmpv[:, 0:32], -16.0, None, ALU.mult, None, ["small"], ["small"])
        barrier(["wm0", "wm1", "Gm", "modall"] + XK)


    def load_xT():
        n = 0
        for t in range(8):
            s = t % 2
            dma("sp", STG[s], xin[t * 128:(t + 1) * 128, :], [], [STGK[s]])
            for c4 in range(4):
                bank, bk = BANKS2[n % 2]
                n += 1
                for cc in range(4):
                    c = c4 * 4 + cc
                    tr(bank[:, cc * 128:(cc + 1) * 128], STG[s][:, c * 128:(c + 1) * 128], ident[:],
                       [STGK[s], "ident"], bk, cc == 3)
                eng = "act" if c4 % 2 == 0 else "dve"
                cp(eng, xT[:, c4 * 4:(c4 + 1) * 4, t * 128:(t + 1) * 128],
                   bank[:].rearrange("p (c n) -> p c n", n=128), bk, [f"xT{c4}"])

    def norm_mod(l, n):
        for g in range(2):
            cols = slice(g * HALF, (g + 1) * HALF)
            import os as _os
            _k = _os.environ.get("KDBG", "")
            for c in range(16):
                s = c % 2
                if "n1" in _k and (c > 0 or g > 0):
                    continue
                if "n4" in _k and (c > 3 or g > 0):
                    continue
                if "wrrst" in _k:
                    tt("dve", rsT[:], xT[:, c, cols], xT[:, c, cols], ALU.mult, [f"xT{c // 4}"], ["rsT"])
                elif "rdsmall" in _k:
                    tt("dve", sqb[s][:], small[:, 0:512], small[:, 0:512], ALU.mult, ["small"], [f"sqb{s}"])
                elif "sqdve" in _k:
                    tt("dve", sqb[s][:], xT[:, c, cols], xT[:, c, cols], ALU.mult, [f"xT{c // 4}"], [f"sqb{s}"])
                elif "sqcopy" in _k:
                    act(sqb[s][:], xT[:, c, cols], AF.Copy, [f"xT{c // 4}"], [f"sqb{s}"])
                else:
                    act(sqb[s][:], xT[:, c, cols], AF.Square, [f"xT{c // 4}"], [f"sqb{s}"])
                if "nomm" not in _k:
                    mm(p_x[:], onesb[:], sqb[s][:], c == 0, c == 15, [f"sqb{s}", "onesb"], ["p_x"], True)
            if "nomm" in _k or "noact2" in _k:
                continue
            act(rsT[:], p_x[:], AF.Sqrt, ["p_x", "small"], ["rsT"], bias=eps_s, scale=1.0 / D)
            recip(rsT[:], rsT[:], ["rsT"], ["rsT"])
            if stage in (0.4, 0.35):
                continue
            sh = mod_vec(g, l, 3 * n)
            for c in range(16):
                tmp = sgt[:, c % 2, :]
                tt("dve", tmp, xT[:, c, cols], rsT[:], ALU.mult, [f"xT{c // 4}", "rsT"], [f"sgt{c % 2}"])
                act(hT[:, c, cols], tmp, AF.Identity, [f"sgt{c % 2}", "small"], [f"hT{g}"],
                    bias=sh[:, c:c + 1], scale=affA4[:, g, l, n, c:c + 1])

    T = [xflat[:, 8192 + i * 512:8192 + (i + 1) * 512] for i in range(8)]
    olocb = xflat[:, 0:4096].bitcast(BF16).rearrange("p (u t) -> p u t", t=HALF)
    gstash = xflat[:, 4096:8192].bitcast(BF16).rearrange("p (u t) -> p u t", t=HALF)
    Sin = xflat[:, 12288:16384].rearrange("p (d f) -> p d f", d=2)
    SinB = xflat[:, 8192:10240].bitcast(BF16).rearrange("p (d f) -> p d f", d=2)
    gbuf = xflat[:, 10240:11264]
    dall = xflat[:, 11264:11520].rearrange("p (r f) -> p r f", f=32)
    dprf = xflat[:, 11520:12032]
    dpr = dprf.rearrange("p (d r f) -> p d r f", d=2, r=8)

    ucount = [0]

    def gla_dir(g, uidx, di, qs, k_ap, g_ap, nh, keys):
        fwd = di == 0
        segs = [(0, 16)] if g == 1 else [(0, 8), (8, 8)]
        rm = rF if fwd else rB

        def rv(ap):
            return ap if fwd else ap[:, ::-1]
        b = T[0]
        scan(rv(b), rv(rm[:]), rv(g_ap), 0.0, keys + ["rF", "rB"], [TK[0]])
        ep = T[1]
        act(ep, b, AF.Exp, [TK[0]], [TK[1]])
        tt("dve", Qd[:], qs, ep, ALU.mult, keys + [TK[1]], ["Qd"])
        em = T[2]
        act(em, b, AF.Exp, [TK[0]], [TK[2]], scale=-1.0)
        tt("dve", Ki[:], k_ap, em, ALU.mult, keys + [TK[2]], ["Ki"])
        b3 = b.rearrange("p (c j) -> p c j", j=32)
        li = 31 if fwd else 0
        dtl = T[2]
        tt("dve", dtl.rearrange("p (c j) -> p c j", j=32), b3[:, :, li:li + 1].to_broadcast([128, 16, 32]), b3,
           ALU.subtract, [TK[0]], [TK[2]])
        act(dtl, dtl, AF.Exp, [TK[2]], [TK[2]])
        tt("dve", Kt[:], k_ap, dtl, ALU.mult, keys + [TK[2]], ["Kt"])
        ep3 = ep.rearrange("p (c j) -> p c j", j=32)
        if g == 1:
            bf = T[3]
            scan(rv(bf), rv(ones32[:]), rv(g_ap), 0.0, keys + ["ones32"], [TK[3]])
            act(bf, bf, AF.Exp, [TK[3]], [TK[3]])
            slot = uidx * 2 + di
            tt("dve", qst[:, slot, :], qs, bf, ALU.mult, keys + [TK[3]], [QK[slot]])
            lc = HALF - 1 if fwd else 0
            cp("dve", dvec[:, slot:slot + 1], bf[:, lc:lc + 1], [TK[3]], ["dvec"])
        for t in range(4):
            tr(p_tr[:, t * 128:(t + 1) * 128], Kt[:, t * 128:(t + 1) * 128], identb[:], ["Kt", "identb"], ["p_tr"], t == 3)
        for r in range(4):
            eng = "dve" if r % 2 == 0 else "pool"
            if eng == "pool":
                act(KtM[:, r, :], p_tr[:, 0:512], AF.Copy, ["p_tr", "small"], ["KtM"], scale=rm_s[:, r:r + 1])
            else:
                ts("dve", KtM[:, r, :], p_tr[:, 0:512], rm_s[:, r:r + 1], None, ALU.mult, None, ["p_tr", "small"], ["KtM"])
        for t in range(4):
            mm(p_att[:, t * 128:(t + 1) * 128], Ki[:, t * 128:(t + 1) * 128], Qd[:, t * 128:(t + 1) * 128],
               True, True, ["Ki", "Qd"], ["p_att"], t == 3)
        tt("dve", attm[:], p_att[:], (mF if fwd else mB)[:], ALU.mult, ["p_att", "mF", "mB"], ["attm"])
        for h in range(nh):
            order = []
            for (c0, ncn) in segs:
                cs = list(range(c0, c0 + ncn))
                order.append(cs if fwd else cs[::-1])
            for si, cs in enumerate(order):
                cur = None
                groups = []
                for c in cs:
                    if groups and groups[-1][0] == c // 4:
                        groups[-1][1].append(c)
                    else:
                        groups.append((c // 4, [c]))
                idx = 0
                for (t, cl) in groups:
                    bi = ucount[0] % 2
                    ucount[0] += 1
                    ubk = f"p_u{bi}"
                    for k_, c in enumerate(cl):
                        r = c % 4
                        mm(p_u[bi][:, r * 128:(r + 1) * 128], KtM[:, r, t * 128:(t + 1) * 128],
                           vTk[:, h, t * 128:(t + 1) * 128], True, True, ["KtM", "vTk"], [ubk], k_ == len(cl) - 1)
                    for c in cl:
                        r = c % 4
                        U = p_u[bi][:, r * 128:(r + 1) * 128]
                        nxt = S32[idx % 2]
                        nk_ = f"S32_{idx % 2}"
                        if cur is None:
                            cp("dve", nxt[:], U, [ubk], [nk_])
                        else:
                            stt(nxt[:], cur[0][:], ep3[:, c, li:li + 1], U, ALU.mult, ALU.add, [cur[1], TK[1], ubk], [nk_])
                        cur = (nxt, nk_)
                        pos = cs.index(c)
                        if pos < len(cs) - 1:
                            cnext = cs[pos + 1]
                            cp("act", Sst[:, cnext, :], nxt[:], [nk_], [f"Sst{cnext}"])
                        idx += 1
                if g == 0:
                    if nh == 1:
                        dst = nsa[si, di, uidx]
                    else:
                        dst = nsb[si, di, uidx - 8][:, h * 128:(h + 1) * 128]
                    dma("sp", dst, cur[0][:], [cur[1]], ["nso"])
                else:
                    if nh == 1:
                        off = di * 1024 + uidx * 128
                    else:
                        off = 2048 + di * 1024 + (uidx - 8) * 256 + h * 128
                    dma("pool", bn_s0_i.ap()[:, off:off + 128], cur[0][:], [cur[1]], ["bn_s0_i"])
            firsts = set(cs[0] for cs in order)
            mlist = []
            for t in range(4):
                cl = [c for c in range(4 * t, 4 * t + 4) if c not in firsts]
                mlist.append((p_o[:, t * 128:(t + 1) * 128], vTk[:, h, t * 128:(t + 1) * 128], attm[:, t * 128:(t + 1) * 128],
                              True, len(cl) == 0, ["vTk", "attm"]))
                for i, c in enumerate(cl):
                    mlist.append((p_o[:, c * 32:(c + 1) * 32], Sst[:, c, :], Qd[:, c * 32:(c + 1) * 32], False,
                                  i == len(cl) - 1, [f"Sst{c}", "Qd"]))
            for i, (o_, l_, r_, st_, sp_, rk) in enumerate(mlist):
                mm(o_, l_, r_, st_, sp_, rk, ["p_o"], i == len(mlist) - 1)
            if fwd:
                cp("act", osb[:, h, :], p_o[:], ["p_o"], [f"osb{h}"])
            else:
                tt("dve", osb[:, h, :], osb[:, h, :], p_o[:], ALU.add, ["p_o", f"osb{h}"], [f"osb{h}"])

    def head_norm(nh, gains, dst_fn, wkeys_fn):
        for h in range(nh):
            act(sqb[h][:], osb[:, h, :], AF.Square, [f"osb{h}"], [f"sqb{h}"])
            mm(p_x[:], onesb[:], sqb[h][:], h == 0, h == nh - 1, [f"sqb{h}", "onesb"], ["p_x"], True)
        act(rsT[:], p_x[:], AF.Sqrt, ["p_x", "small"], ["rsT"], bias=eps_s, scale=1.0 / (128 * nh))
        recip(rsT[:], rsT[:], ["rsT"], ["rsT"])
        for h in range(nh):
            tt("dve", osb[:, h, :], osb[:, h, :], rsT[:], ALU.mult, [f"osb{h}", "rsT"], [f"osb{h}"])
            stt(dst_fn(h), osb[:, h, :], gains[h], sgt[:, h, :], ALU.mult, ALU.mult,
                [f"osb{h}", f"sgt{h}", "small"], wkeys_fn(h))

    def evac_vT(h, src_ps, pk):
        cp("act", vT2[:, h, :], src_ps, pk, [f"vT2{h}"])
        for t in range(4):
            tr(p_tr[:, 512 + t * 128:512 + (t + 1) * 128], vT2[:, h, t * 128:(t + 1) * 128], identb[:],
               [f"vT2{h}", "identb"], ["p_tr"], t == 3)
        cp("dve", vTk[:, h, :], p_tr[:, 512:1024], ["p_tr"], ["vTk"])

    def layer0_mixer():
        plan = []
        for g in (1, 0):
            for hh in range(8):
                for nm in ("aq", "af_f", "af_b", "ai", "ag"):
                    plan.append(w_slab(w_in0, A_COLS[nm] + 128 * hh))
            plan.append(w_slab(w_in0, B_COLS["blr"], 32))
            for hb in range(4):
                plan.append(w_slab(w_in0, B_COLS["bq"] + 128 * hb))
                plan.append(w_slab(w_in0, B_COLS["bk"] + 128 * hb))
                for i in range(2):
                    plan.append(w_slab(w_in0, B_COLS["bv"] + 256 * hb + 128 * i))
                for i in range(2):
                    plan.append(w_slab(w_in0, B_COLS["bg"] + 256 * hb + 128 * i))
        WS.extend(plan)
        barrier(XK + TK + ["oloc", "gstash", "Sin"])
        pjn = [0]

        def pr(g):
            cols = slice(g * HALF, (g + 1) * HALF)
            bank, bk = BANKS2[pjn[0] % 2]
            pjn[0] += 1
            projm([(lambda kc: hT[:, kc, cols], bank, [f"hT{g}"], bk)])
            return bank, bk

        for g in (1, 0):
            for hh in range(8):
                qs, sgf, sgb_ = T[4], T[5], T[6]
                bank, bk = pr(g)
                act(qs, bank[:], AF.Copy, bk, [TK[4]], scale=128.0 ** -0.5)
                bank, bk = pr(g)
                act(sgf, bank[:], AF.Sigmoid, bk, [TK[5]])
                bank, bk = pr(g)
                act(sgb_, bank[:], AF.Sigmoid, bk, [TK[6]])
                bank, bk = pr(g)
                evac_vT(0, bank[:], bk)
                bank, bk = pr(g)
                act(sgt[:, 0, :], bank[:], AF.Silu, bk, ["sgt0"])
                for di, sg in enumerate((sgf, sgb_)):
                    sk = TK[5 + di]
                    gl = T[7]
                    act(gl, sg, AF.Ln, [sk, "small"], [TK[7]], bias=lb0[:, hh:hh + 1], scale=oml[:, hh:hh + 1])
                    ts("dve", sg, sg, noml[:, hh:hh + 1], oml[:, hh:hh + 1], ALU.mult, ALU.add, [sk, "small"], [sk])
                    gla_dir(g, hh, di, qs, sg, gl, 1, [TK[4], sk, TK[7]])
                if g == 0:
                    head_norm(1, [agn_s[:, 0:1]], lambda h, hh=hh: hT[:, hh, HALF:NT], lambda h: ["hT1"])
                else:
                    cp("dve", olocb[:, hh, :], osb[:, 0, :], ["osb0"], ["oloc"])
                    cp("act", gstash[:, hh, :], sgt[:, 0, :], ["sgt0"], ["gstash"])
            bank, bk = pr(g)
            cp("act", blrT[:], bank[0:32, :], bk, ["blrT"])
            for hb in range(4):
                u = 8 + hb
                qs, kk = T[4], T[5]
                bank, bk = pr(g)
                act(qs, bank[:], AF.Copy, bk, [TK[4]], scale=128.0 ** -0.5)
                bank, bk = pr(g)
                cp("act", kk, bank[:], bk, [TK[5]])
                for i in range(2):
                    bank, bk = pr(g)
                    evac_vT(i, bank[:], bk)
                for i in range(2):
                    bank, bk = pr(g)
                    act(sgt[:, i, :], bank[:], AF.Silu, bk, [f"sgt{i}"])
                for di in range(2):
                    gl = T[7]
                    mm(p_x[:], wal[:, di, hb * 128:(hb + 1) * 128], blrT[:], True, True, ["wal", "blrT"], ["p_x"], True)
                    act(gl, p_x[:], AF.Exp, ["p_x", "small"], [TK[7]], bias=nbal[:, di * 4 + hb:di * 4 + hb + 1], scale=-1.0)
                    act(gl, gl, AF.Ln, [TK[7], "small"], [TK[7]], bias=one_s)
                    ts("dve", gl, gl, -1.0 / 16.0, None, ALU.mult, None, [TK[7]], [TK[7]])
                    gla_dir(g, u, di, qs, kk, gl, 2, [TK[4], TK[5], TK[7]])
                if g == 0:
                    head_norm(2, [bgn_s[:, 0:1], bgn_s[:, 1:2]],
                              lambda h, hb=hb: hT[:, 8 + 2 * hb + h, HALF:NT], lambda h: ["hT1"])
                else:
                    for h in range(2):
                        mc = 8 + 2 * hb + h
                        cp("dve", olocb[:, mc, :], osb[:, h, :], [f"osb{h}"], ["oloc"])
                        cp("act", gstash[:, mc, :], sgt[:, h, :], [f"sgt{h}"], ["gstash"])
            if g == 1:
                dma("pool", bn_s0_i.ap()[:, 4096:4160], dvec, ["dvec"], ["bn_s0_i"])
                allgather(bn_s0_i, bn_s0_o, ["bn_s0_i"], ["bn_s0_o"])

        barrier(TK + ["SinB", "gbuf", "dall", "dpr"])
        for d_ in range(2):
            dma("sp", Sin[:, d_, 0:1024].rearrange("p (h v) -> p h v", v=128),
                st_a[d_].rearrange("h k v -> k h v"), [], ["Sin"])
            dma("sp", Sin[:, d_, 1024:2048].rearrange("p (h v) -> p h v", v=256),
                st_b[d_].rearrange("h k v -> k h v"), [], ["Sin"])
        gout = bn_s0_o.ap()
        dma("pool", dall, gout[:, 4096:4128].rearrange("(r p) f -> p r f", p=128), ["bn_s0_o"], ["dall"])
        for di in range(2):
            for r in range(8):
                ts("dve", dpr[:, di, r, :], dall[:, r, :], -1.0, msk_s[:, di * 8 + r:di * 8 + r + 1],
                   ALU.add, ALU.mult, ["dall", "small"], ["dpr"])
        ts("dve", dprf, dprf, 1.0, None, ALU.add, None, ["dpr"], ["dpr"])
        for di in range(2):
            rorder = list(range(8)) if di == 0 else list(range(7, -1, -1))
            for r in rorder:
                for part in range(2):
                    c0 = part * 2048 + di * 1024
                    dma("pool", gbuf, gout[r * 128:(r + 1) * 128, c0:c0 + 1024], ["bn_s0_o"], ["gbuf"])
                    ts("dve", gbuf, gbuf, msk_s[:, di * 8 + r:di * 8 + r + 1], None, ALU.mult, None, ["gbuf", "small"], ["gbuf"])
                    nu, dv = (8, 128) if part == 0 else (4, 256)
                    for h in range(nu):
                        u = h if part == 0 else 8 + h
                        sl = Sin[:, di, part * 1024 + h * dv:part * 1024 + (h + 1) * dv]
                        stt(sl, sl, dpr[:, di, r, u * 2 + di:u * 2 + di + 1], gbuf[:, h * dv:(h + 1) * dv],
                            ALU.mult, ALU.add, ["Sin", "dpr", "gbuf"], ["Sin"])
        for di in range(2):
            cp("act", SinB[:, di, :], Sin[:, di, :], ["Sin"], ["SinB"])

        for u in range(12):
            nh = 1 if u < 8 else 2
            for h in range(nh):
                for di in range(2):
                    if u < 8:
                        sl = SinB[:, di, u * 128:(u + 1) * 128]
                    else:
                        o = 1024 + (u - 8) * 256 + h * 128
                        sl = SinB[:, di, o:o + 128]
                    mm(p_o[:], sl, qst[:, u * 2 + di, :], di == 0, di == 1, ["SinB", QK[u * 2 + di]], ["p_o"], di == 1)
                mc = u if u < 8 else 8 + 2 * (u - 8) + h
                tt("dve", osb[:, h, :], olocb[:, mc, :], p_o[:], ALU.add, ["p_o", "oloc"], [f"osb{h}"])
                cp("act", sgt[:, h, :], gstash[:, mc, :], ["gstash"], [f"sgt{h}"])
            if u < 8:
                head_norm(1, [agn_s[:, 0:1]], lambda h, u=u: qst[:, u, :], lambda h, u=u: [QK[u]])
            else:
                head_norm(2, [bgn_s[:, 0:1], bgn_s[:, 1:2]],
                          lambda h, u=u: qst[:, 8 + 2 * (u - 8) + h, :], lambda h, u=u: [QK[8 + 2 * (u - 8) + h]])

    def out_proj_residual(w, l, gate_j, src_fn, src_keys, nkc):
        WS.extend([w_slab(w, ob * 128, 128, 0, nkc) for ob in range(16)])
        n = 0
        for ob in range(16):
            tg = []
            for g in range(2):
                bank, bk = BANKS6[n % 6]
                n += 1
                tg.append((lambda kc, g=g: src_fn(kc, g), bank, src_keys(g), bk))
            projm(tg)
            for g in range(2):
                cols = slice(g * HALF, (g + 1) * HALF)
                _, bank, _, bk = tg[g]
                gt = mod_vec(g, l, gate_j)
                stt(xT[:, ob, cols], bank[:], gt[:, ob:ob + 1], xT[:, ob, cols], ALU.mult, ALU.add,
                    bk + ["small", f"xT{ob // 4}"], [f"xT{ob // 4}"])

    def ffn(l):
        w1 = fw_in[l]
        w2 = fw_out[l]
        plan = []
        for grp in range(4):
            for j in range(11):
                f = grp * 11 + j
                plan.append(w_slab(w1, f * 128))
                plan.append(w_slab(w1, DFF + f * 128))
            for ob in range(16):
                plan.append(w_slab(w2, ob * 128, 128, grp * 11, 11))
        WS.extend(plan)
        n = 0
        for grp in range(4):
            for j in range(11):
                for which in range(2):
                    tg = []
                    for g in range(2):
                        cols = slice(g * HALF, (g + 1) * HALF)
                        bank, bk = BANKS6[n % 6]
                        n += 1
                        tg.append((lambda kc, cols=cols: hT[:, kc, cols], bank, [f"hT{g}"], bk))
                    projm(tg)
                    for g in range(2):
                        cols = slice(g * HALF, (g + 1) * HALF)
                        _, bank, _, bk = tg[g]
                        if which == 0:
                            act(sgt[:, g, :], bank[:], AF.Silu, bk, [f"sgt{g}"])
                        else:
                            tt("dve", mT[:, j, cols], bank[:], sgt[:, g, :], ALU.mult, bk + [f"sgt{g}"], [f"mT{g}"])
            for ob in range(16):
                tg = []
                for g in range(2):
                    cols = slice(g * HALF, (g + 1) * HALF)
                    bank, bk = BANKS6[n % 6]
                    n += 1
                    tg.append((lambda kc, cols=cols: mT[:, kc, cols], bank, [f"mT{g}"], bk))
                projm(tg)
                for g in range(2):
                    cols = slice(g * HALF, (g + 1) * HALF)
                    _, bank, _, bk = tg[g]
                    gt = mod_vec(g, l, 5)
                    stt(xT[:, ob, cols], bank[:], gt[:, ob:ob + 1], xT[:, ob, cols], ALU.mult, ALU.add,
                        bk + ["small", f"xT{ob // 4}"], [f"xT{ob // 4}"])

    def layer1_mixer():
        xc32 = xT[:, 0:2, :]
        gel = xT[:, 2:4, :]
        tA = xT[:, 4, :]
        tB = xT[:, 5, :]
        ybuf = xT[:, 6:8, :]
        P12 = xflat[:, 8192:16384].bitcast(BF16).rearrange("p (a t) -> p a t", t=HALF)
        plan = []
        for nb in range(8):
            for ci in range(2):
                c = nb * 2 + ci
                plan.append(w_slab(w_in1, c * 128))
                plan.append(w_slab(w_in1, D + c * 128))
            for d in range(2):
                for oc in range(2):
                    for wt_ in (rgw, igw):
                        plan.append((wt_[d, nb][:, oc * 128:(oc + 1) * 128], 2, 128))
        WS.extend(plan)
        n = [0]

        def two_banks():
            out = []
            for g in range(2):
                bank, bk = BANKS6[n[0] % 6]
                n[0] += 1
                out.append((bank, bk))
            return out
        cw3 = cw_s.rearrange("p (j c) -> p j c", j=4)
        rgb3 = rgb_s.rearrange("p (d c) -> p d c", d=2)
        igb3 = igb_s.rearrange("p (d c) -> p d c", d=2)
        c13 = c1_s.rearrange("p (d c) -> p d c", d=2)
        c23 = c2_s.rearrange("p (d c) -> p d c", d=2)
        r1s = r1sum.rearrange("p (d c x) -> p d c x", d=2, x=2)
        s1 = [mF[:], mB[:]]
        hs = [rF[:], rB[:]]
        for nb in range(8):
            for ci in range(2):
                c = nb * 2 + ci
                bb = two_banks()
                projm([(lambda kc, g=g: hT[:, kc, g * HALF:(g + 1) * HALF], bb[g][0], [f"hT{g}"], bb[g][1]) for g in range(2)])
                for g in range(2):
                    cols = slice(g * HALF, (g + 1) * HALF)
                    bank, bk = bb[g]
                    L = 256 if g == 0 else 64
                    xr = tA[:, cols]
                    cp("act", xr, bank[:], bk, [f"tA{g}"])
                    xo = xc32[:, ci, cols]
                    ts("dve", xo, xr, cw3[:, 2, c:c + 1], cb_s[:, c:c + 1], ALU.mult, ALU.add, [f"tA{g}", "small"], [f"xc{ci}{g}"])
                    xo3 = xo.rearrange("p (s t) -> p s t", t=L)
                    xr3 = xr.rearrange("p (s t) -> p s t", t=L)
                    for j, sh in ((0, -2), (1, -1), (3, 1)):
                        if sh < 0:
                            o_, i_ = xo3[:, :, -sh:L], xr3[:, :, 0:L + sh]
                        else:
                            o_, i_ = xo3[:, :, 0:L - sh], xr3[:, :, sh:L]
                        stt(o_, i_, cw3[:, j, c:c + 1], o_, ALU.mult, ALU.add, [f"tA{g}", "small", f"xc{ci}{g}"], [f"xc{ci}{g}"])
                    cp("act", xcb[:, ci, cols], xo, [f"xc{ci}{g}"], [f"xcb{g}"])
                bb = two_banks()
                projm([(lambda kc, g=g: hT[:, kc, g * HALF:(g + 1) * HALF], bb[g][0], [f"hT{g}"], bb[g][1]) for g in range(2)])
                for g in range(2):
                    cols = slice(g * HALF, (g + 1) * HALF)
                    bank, bk = bb[g]
                    gx = tA[:, cols]
                    cp("act", gx, bank[:], bk, [f"tA{g}"])
                    u_ = tB[:, cols]
                    tt("dve", u_, gx, gx, ALU.mult, [f"tA{g}"], [f"tB{g}"])
                    ts("dve", u_, u_, 0.044715, 1.0, ALU.mult, ALU.add, [f"tB{g}"], [f"tB{g}"])
                    tt("dve", u_, u_, gx, ALU.mult, [f"tA{g}", f"tB{g}"], [f"tB{g}"])
                    act(u_, u_, AF.Sigmoid, [f"tB{g}"], [f"tB{g}"], scale=1.5957691216057308)
                    tt("dve", gel[:, ci, cols], u_, gx, ALU.mult, [f"tA{g}", f"tB{g}"], [f"gel{ci}{g}"])
            for d in range(2):
                fwd = d == 0
                for oc in range(2):
                    c = nb * 2 + oc
                    for gi, bias3 in enumerate((rgb3, igb3)):
                        bb = two_banks()
                        projm([(lambda kc, g=g: xcb[:, kc, g * HALF:(g + 1) * HALF], bb[g][0], [f"xcb{g}"], bb[g][1]) for g in range(2)])
                        for g in range(2):
                            cols = slice(g * HALF, (g + 1) * HALF)
                            bank, bk = bb[g]
                            dstt = tA if gi == 0 else tB
                            dk_ = f"tA{g}" if gi == 0 else f"tB{g}"
                            act(dstt[:, cols], bank[:], AF.Sigmoid, bk + ["small"], [dk_], bias=bias3[:, d, c:c + 1])
                    for g in range(2):
                        cols = slice(g * HALF, (g + 1) * HALF)
                        act(s1[g], tA[:, cols], AF.Exp, [f"tA{g}", "small"], [f"s1{g}"], scale=c23[:, d, c:c + 1])
                        act(s1[g], s1[g], AF.Sqrt, [f"s1{g}", "small"], [f"s1{g}"], bias=one_s, scale=-1.0)
                        tt("dve", tB[:, cols], tB[:, cols], xc32[:, oc, cols], ALU.mult, [f"tB{g}", f"xc{oc}{g}"], [f"tB{g}"])
                        tt("dve", tB[:, cols], tB[:, cols], s1[g], ALU.mult, [f"tB{g}", f"s1{g}"], [f"tB{g}"])
                        act(tA[:, cols], tA[:, cols], AF.Exp, [f"tA{g}", "small"], [f"tA{g}"], scale=c13[:, d, c:c + 1])
                    for si, (g, a0, a1) in enumerate(((0, 0, 256), (0, 256, 512), (1, 0, 512))):
                        sl = slice(g * HALF + a0, g * HALF + a1)
                        o_ = hs[g][:, a0:a1]
                        av, uv = tA[:, sl], tB[:, sl]
                        if not fwd:
                            o_, av, uv = o_[:, ::-1], av[:, ::-1], uv[:, ::-1]
                        scan(o_, av, uv, 0.0, [f"tA{g}", f"tB{g}"], [f"hs{g}"])
                        lc = (a1 - 1) if fwd else a0
                        if g == 0:
                            col = (si * 2 + d) * 16 + c
                            cp("act", nsrs[:, col:col + 1], hs[0][:, lc:lc + 1], ["hs0"], ["nsrs"])
                        else:
                            cp("act", r1s[:, d, c, 1:2], hs[1][:, lc:lc + 1], ["hs1"], ["r1sum"])
                    acum = s1[1]
                    av = tA[:, HALF:NT]
                    o_ = acum
                    if not fwd:
                        av, o_ = av[:, ::-1], o_[:, ::-1]
                    scan(o_, av, ones32[:], 1.0, ["tA1", "ones32"], ["s11"], op0=ALU.mult, op1=ALU.mult)
                    lc = HALF - 1 if fwd else 0
                    cp("act", r1s[:, d, c, 0:1], acum[:, lc:lc + 1], ["s11"], ["r1sum"])
                    ysl = ybuf[:, oc, :]
                    for g in range(2):
                        cols = slice(g * HALF, (g + 1) * HALF)
                        if fwd:
                            cp("act", ysl[:, cols], hs[g], [f"hs{g}"], [f"yb{oc}{g}"])
                        else:
                            tt("dve", ysl[:, cols], ysl[:, cols], hs[g], ALU.add, [f"hs{g}", f"yb{oc}{g}"], [f"yb{oc}{g}"])
                    tt("dve", P12[:, d * 16 + c, :], acum, gel[:, oc, HALF:NT], ALU.mult, ["s11", f"gel{oc}1"], [f"P12_{d}_{c}"])
            for oc in range(2):
                c = nb * 2 + oc
                for g in range(2):
                    cols = slice(g * HALF, (g + 1) * HALF)
                    tt("dve", mT[:, c, cols], ybuf[:, oc, cols], gel[:, oc, cols], ALU.mult,
                       [f"yb{oc}{g}", f"gel{oc}{g}"], [f"mT{g}"])
        dma("sp", nsr, nsrs, ["nsrs"], ["nso"])
        dma("pool", bn_s1_i.ap(), r1sum, ["r1sum"], ["bn_s1_i"])
        allgather(bn_s1_i, bn_s1_o, ["bn_s1_i"], ["bn_s1_o"])
        gall = xT[:, 4, 0:512]
        dma("pool", gall.rearrange("p (r f) -> p r f", f=64), bn_s1_o.ap().rearrange("(r p) f -> p r f", p=128),
            ["bn_s1_o"], ["tA0"])
        g5 = gall.rearrange("p (r d c x) -> p r d c x", d=2, c=16, x=2)
        cp("dve", hin, str_s, ["small"], ["small"])
        hin3 = hin.rearrange("p (d c) -> p d c", d=2)
        for d in range(2):
            rorder = list(range(8)) if d == 0 else list(range(7, -1, -1))
            for r in rorder:
                m = msk_s[:, d * 8 + r:d * 8 + r + 1]
                av = tmpv[:, 0:16]
                hv = tmpv[:, 16:32]
                ts("dve", av, g5[:, r, d, :, 0], -1.0, m, ALU.add, ALU.mult, ["tA0", "small"], ["small"])
                ts("dve", av, av, 1.0, None, ALU.add, None, ["small"], ["small"])
                ts("dve", hv, g5[:, r, d, :, 1], m, None, ALU.mult, None, ["tA0", "small"], ["small"])
                tt("dve", hin3[:, d, :], hin3[:, d, :], av, ALU.mult, ["small"], ["small"])
                tt("dve", hin3[:, d, :], hin3[:, d, :], hv, ALU.add, ["small"], ["small"])
        for c in range(16):
            for d in range(2):
                stt(mT[:, c, HALF:NT], P12[:, d * 16 + c, :], hin3[:, d, c:c + 1], mT[:, c, HALF:NT], ALU.mult, ALU.add,
                    [f"P12_{d}_{c}", "small", "mT1"], ["mT1"])

    def final_out():
        for g in range(2):
            cols = slice(g * HALF, (g + 1) * HALF)
            for c in range(16):
                s = c % 2
                act(sqb[s][:], xT[:, c, cols], AF.Square, [f"xT{c // 4}"], [f"sqb{s}"])
                mm(p_x[:], onesb[:], sqb[s][:], c == 0, c == 15, [f"sqb{s}", "onesb"], ["p_x"], True)
            act(rsT[:], p_x[:], AF.Sqrt, ["p_x", "small"], ["rsT"], bias=eps_s, scale=1.0 / D)
            recip(rsT[:], rsT[:], ["rsT"], ["rsT"])
            for c in range(16):
                stt(xT[:, c, cols], xT[:, c, cols], fng_s[:, c:c + 1], rsT[:], ALU.mult, ALU.mult,
                    [f"xT{c // 4}", "rsT", "small"], [f"xT{c // 4}"])
        n = 0
        for t in range(8):
            s = t % 2
            for c4 in range(4):
                bank, bk = BANKS6[n % 6]
                n += 1
                for cc in range(4):
                    c = c4 * 4 + cc
                    tr(bank[:, cc * 128:(cc + 1) * 128], xT[:, c, t * 128:(t + 1) * 128], ident[:],
                       [f"xT{c // 4}", "ident"], bk, cc == 3)
                eng = "act" if c4 % 2 == 0 else "dve"
                cp(eng, STG[s][:, c4 * 512:(c4 + 1) * 512], bank[:], bk, [STGK[s]])
            dma("sp", y_out[t * 128:(t + 1) * 128, :], STG[s], [STGK[s]], ["yout"])

    ALLP = ["pj0", "pj1", "p_att", "p_o"] + PUK[0] + PUK[1]
    MK = ["mT0", "mT1"]

    def dump():
        allk = list(P.res.keys())
        barrier(allk)
        dma("sp", dbg_small, small[:], allk, ["dbg"])
        dma("sp", dbg_h, hT[:].rearrange("p c t -> p (c t)"), allk, ["dbg"])
        dma("sp", dbg_x, xflat, allk, ["dbg"])
        if stage == 1:
            mset("dve", qflat, 0.0, allk)
        else:
            mset("dve", qflat[:, 24 * HALF:32 * HALF], 0.0, allk)
        dma("sp", dbg_q, qflat, allk, ["dbg"])
        P.finish()

    def program():
        if stage == 0.1:
            return dump()
        if stage == 0.35:
            barrier(QK + STGK)
            load_xT()
            norm_mod(0, 0)
            return dump()
        modulation()
        if stage == 0.2:
            return dump()
        if stage == 0.36:
            barrier(QK + STGK)
            load_xT()
            norm_mod(0, 0)
            return dump()
        barrier(QK + STGK)
        load_xT()
        if stage == 0.3:
            return dump()
        if stage == 0.4:
            norm_mod(0, 0)
            return dump()
        norm_mod(0, 0)
        if stage == 1:
            return dump()
        layer0_mixer()
        if stage == 2:
            return dump()
        barrier(XK + TK + ["oloc", "gstash", "Sin", "SinB", "gbuf", "dall", "dpr"] + QK[16:] + STGK)
        load_xT()
        out_proj_residual(w_out0, 0, 2, lambda kc, g: (hT[:, kc, HALF:NT] if g == 0 else qst[:, kc, :]),
                          lambda g: ["hT1"] if g == 0 else QK[0:16], 16)
        if stage == 3:
            return dump()
        norm_mod(0, 1)
        barrier(QK + MK + STGK)
        ffn(0)
        if stage == 4:
            return dump()
        norm_mod(1, 0)
        dma("sp", xspill.ap(), xflat, XK, ["xspill"])
        L1K = ([f"tA{g}" for g in range(2)] + [f"tB{g}" for g in range(2)] + [f"xc{ci}{g}" for ci in range(2) for g in range(2)]
               + [f"gel{ci}{g}" for ci in range(2) for g in range(2)] + [f"yb{ci}{g}" for ci in range(2) for g in range(2)]
               + [f"P12_{d}_{c}" for d in range(2) for c in range(16)])
        barrier(XK + L1K + MK + ["mF", "mB", "rF", "rB", "s10", "s11", "hs0", "hs1"])
        layer1_mixer()
        barrier(XK + L1K)
        dma("sp", xflat, xspill.ap(), ["xspill"], XK)
        if stage == 5:
            return dump()
        out_proj_residual(w_out1, 1, 2, lambda kc, g: mT[:, kc, g * HALF:(g + 1) * HALF],
                          lambda g: [f"mT{g}"], 16)
        norm_mod(1, 1)
        ffn(1)
        barrier(MK + STGK + QK)
        final_out()
        P.finish()

    program()

    sems = {}
    for i in range(P.ncc):
        sems[("cc", i)] = es.enter_context(nc.semaphore(f"s_cc{i}"))
    for e in Prog.ENG:
        sems[e] = es.enter_context(nc.semaphore(f"s_{e}"))
    for i in range(P.ndma):
        sems[("dma", i)] = es.enter_context(nc.semaphore(f"s_dma{i}"))
    nums = sorted(h.num for h in sems.values())
    with nc.Block() as b0:
        @b0.sync
        def _(sp):
            sp.sem_clear(range(nums[0], nums[-1] + 1))
    block = es.enter_context(nc.Block())
    P.emit(nc, block, sems)
    es.close()
    return nc


def _fm(v, nchunk):
    v = np.asarray(v, np.float32)
    lead = v.shape[:-1]
    a = v.reshape(lead + (nchunk, 128))
    a = np.moveaxis(a, -1, 0)
    return np.ascontiguousarray(a.reshape(128, -1))


_CACHE = {}


def _consts():
    j = np.arange(128)[:, None]
    i = np.arange(128)[None, :]
    same = (j // 32) == (i // 32)
    mf = (same & (j <= i)).astype(np.float32)
    mb = (same & (j >= i)).astype(np.float32)
    t = np.arange(512)
    rf = np.broadcast_to((t % 32 != 0).astype(np.float32), (128, 512))
    rb = np.broadcast_to((t % 32 != 31).astype(np.float32), (128, 512))
    rm = ((np.arange(128)[:, None] // 32) == np.arange(4)[None, :]).astype(np.float32)
    return dict(c_rm=np.ascontiguousarray(rm), c_idb=np.eye(128, dtype=np.float32), c_mf=np.ascontiguousarray(np.tile(mf, (1, 4))),
                c_mb=np.ascontiguousarray(np.tile(mb, (1, 4))), c_rf=np.ascontiguousarray(rf),
                c_rb=np.ascontiguousarray(rb))


def _prepare(x_prompt, x_sample, state_hgrn, state_gla, state_rglru, c, c_ctx, norm1_g, norm2_g, w_mod, b_mod,
             ffn_w_in, ffn_w_out, hgrn_lower_bounds, even_w_in, gla_w_alpha, gla_b_alpha, hgrn_norm_g,
             gla_norm_g, even_w_out, odd_w_in, conv_w, conv_b, rg_w, rg_b, ig_w, ig_b, lru_lambda, odd_w_out,
             final_norm_g):
    f = lambda a: np.ascontiguousarray(np.asarray(a, np.float32))
    x_prompt, x_sample = f(x_prompt), f(x_sample)
    consts = _consts()
    cv = np.stack([f(c_ctx), f(c)[0], f(c)[1]], 0)
    cvT = _fm(cv, 16).reshape(128, 3, 16).transpose(0, 2, 1).reshape(128, 48)
    shared = dict(
        cvT=np.ascontiguousarray(cvT),
        n1g=_fm(norm1_g, 16), n2g=_fm(norm2_g, 16), fng=_fm(final_norm_g, 16),
        lbp=np.ascontiguousarray(_fm(hgrn_lower_bounds, 8).reshape(128, 3, 8).transpose(0, 2, 1).reshape(128, 24)),
        w_in0=f(even_w_in)[0], w_al=f(gla_w_alpha)[0], b_al=_fm(f(gla_b_alpha)[0], 4),
        a_gn=_fm(f(hgrn_norm_g)[0], 1), b_gn=_fm(f(gla_norm_g)[0], 2),
        w_out0=f(even_w_out)[0], w_in1=f(odd_w_in)[0],
        cw=_fm(f(conv_w)[0], 16), cb=_fm(f(conv_b)[0], 16),
        rgw=f(rg_w)[0], rgb=_fm(f(rg_b)[0], 16), igw=f(ig_w)[0], igb=_fm(f(ig_b)[0], 16),
        lam=_fm(f(lru_lambda)[0], 16), w_out1=f(odd_w_out)[0],
        fw_in=f(ffn_w_in), fw_out=f(ffn_w_out), **consts)
    w_mod = f(w_mod)
    b_mod = f(b_mod)
    in_maps = []
    for core in range(NCORES):
        s, q = divmod(core, 4)
        xin = np.concatenate([x_prompt[2 * core], x_prompt[2 * core + 1], x_sample[s, q * 512:(q + 1) * 512]], 0)
        selv = np.zeros((128, 2), np.float32)
        selv[:, s] = 1.0
        m = np.zeros((128, 16), np.float32)
        for r in range(8):
            rs_, rq = divmod(r, 4)
            if rs_ == s and rq < q:
                m[:, r] = 1.0
            if rs_ == s and rq > q:
                m[:, 8 + r] = 1.0
        d = dict(shared)
        d.update(
            xin=np.ascontiguousarray(xin), sel=selv, msk=m,
            st_a=f(state_hgrn)[s, 0], st_b=f(state_gla)[s, 0], st_r=_fm(f(state_rglru)[s, 0], 16),
            wmod=np.ascontiguousarray(w_mod[:, :, core * 1536:(core + 1) * 1536]),
            bmod=_fm(b_mod[:, core * 1536:(core + 1) * 1536], 12))
        in_maps.append(d)
    return in_maps


def kernel(x_prompt, x_sample, state_hgrn, state_gla, state_rglru, c, c_ctx, norm1_g, norm2_g, w_mod, b_mod,
           ffn_w_in, ffn_w_out, hgrn_lower_bounds, even_w_in, gla_w_alpha, gla_b_alpha, hgrn_norm_g,
           gla_norm_g, even_w_out, odd_w_in, conv_w, conv_b, rg_w, rg_b, ig_w, ig_b, lru_lambda, odd_w_out,
           final_norm_g):
    in_maps = _prepare(x_prompt, x_sample, state_hgrn, state_gla, state_rglru, c, c_ctx, norm1_g, norm2_g, w_mod, b_mod,
                       ffn_w_in, ffn_w_out, hgrn_lower_bounds, even_w_in, gla_w_alpha, gla_b_alpha, hgrn_norm_g,
                       gla_norm_g, even_w_out, odd_w_in, conv_w, conv_b, rg_w, rg_b, ig_w, ig_b, lru_lambda, odd_w_out,
                       final_norm_g)
    if "nc" not in _CACHE:
        _CACHE["nc"] = build_program()
    nc = _CACHE["nc"]
    res = run_bass_kernel_spmd(nc, in_maps, core_ids=list(range(NCORES)))
    R = res.results
    y_prompt = np.zeros((16, 256, D), np.float32)
    y_sample = np.zeros((2, 2048, D), np.float32)
    ns_h = np.zeros((16, 1, 2, 8, 128, 128), np.float32)
    ns_g = np.zeros((16, 1, 2, 4, 128, 256), np.float32)
    ns_r = np.zeros((16, 1, 2, 2048), np.float32)
    for core in range(NCORES):
        s, q = divmod(core, 4)
        y = R[core]["y"]
        y_prompt[2 * core] = y[0:256]
        y_prompt[2 * core + 1] = y[256:512]
        y_sample[s, q * 512:(q + 1) * 512] = y[512:1024]
        ns_h[2 * core:2 * core + 2, 0] = R[core]["nsa"]
        ns_g[2 * core:2 * core + 2, 0] = R[core]["nsb"]
        r4 = R[core]["nsr"].reshape(128, 2, 2, 16)
        ns_r[2 * core:2 * core + 2, 0] = r4.transpose(1, 2, 3, 0).reshape(2, 2, 2048)
    return (y_prompt, y_sample, ns_h, ns_g, ns_r)
```
